# Optimizing a Trainium2 kernel written in Bass

```python
import jax
import jax.numpy as jnp
from jax import lax
import numpy as np

D_MODEL = 4096
BATCH = 2
SEQ = 8192
DEPTH = 2

D_FF = ((8 * D_MODEL // 3 + 255) // 256) * 256
SC_WIDTH = D_MODEL // 4
SC_KERNEL = 3
CF_WIDTH = D_MODEL // 4
CF_KERNEL = 31
RET_HEADS = 8
RET_WIDTH = D_MODEL // 2
RET_HEAD_DIM = RET_WIDTH // RET_HEADS
RET_CHUNK = 128
ROPE_BASE = 10000.0
EPS = 1e-6
IN_SIZES = (SC_WIDTH,) * 3 + (CF_WIDTH,) * 2 + (RET_WIDTH,) * 4 + (D_MODEL,) * 3
IN_WIDTH = sum(IN_SIZES)
SPLIT_POINTS = tuple(sum(IN_SIZES[:i + 1]) for i in range(len(IN_SIZES) - 1))

kernel_name = 'hybrid_gated_conv_retention_macaron'


def rms_norm(x, w):
    xf = x.astype(jnp.float32)
    y = xf * lax.rsqrt(jnp.mean(xf * xf, axis=-1, keepdims=True) + EPS)
    return (y * w.astype(jnp.float32)).astype(x.dtype)


def layer_norm(x, w, b):
    xf = x.astype(jnp.float32)
    mu = jnp.mean(xf, axis=-1, keepdims=True)
    var = jnp.mean(jnp.square(xf - mu), axis=-1, keepdims=True)
    y = (xf - mu) * lax.rsqrt(var + EPS)
    return (y * w.astype(jnp.float32) + b.astype(jnp.float32)).astype(x.dtype)


def swiglu(h, w_in, w_out):
    gate, up = jnp.split(h @ w_in, 2, axis=-1)
    return (jax.nn.silu(gate) * up) @ w_out


def causal_depthwise_conv(x, w):
    k = w.shape[0]
    return lax.conv_general_dilated(
        x, w[:, None, :].astype(x.dtype), window_strides=(1,), padding=[(k - 1, 0)],
        dimension_numbers=('NWC', 'WIO', 'NWC'), feature_group_count=x.shape[-1])


def rotary(t, positions):
    half = t.shape[-1] // 2
    inv_freq = ROPE_BASE ** (-jnp.arange(half, dtype=jnp.float32) / half)
    ang = positions.astype(jnp.float32)[..., None] * inv_freq
    cos = jnp.cos(ang)[:, :, None, :]
    sin = jnp.sin(ang)[:, :, None, :]
    t1, t2 = t[..., :half], t[..., half:]
    return jnp.concatenate([t1 * cos - t2 * sin, t1 * sin + t2 * cos], axis=-1)


def chunkwise_retention(q, k, v, log_gamma):
    b, s, h, dh = q.shape
    c = RET_CHUNK
    n = s // c
    idx = jnp.arange(c, dtype=jnp.float32)
    rel = idx[:, None] - idx[None, :]
    decay_mask = jnp.exp(jnp.where(rel[None] >= 0, rel[None] * log_gamma[:, None, None], -jnp.inf))
    query_decay = jnp.exp((idx[None, :] + 1.0) * log_gamma[:, None]).T[None, :, :, None]
    key_decay = jnp.exp((c - 1.0 - idx[None, :]) * log_gamma[:, None]).T[None, :, :, None]
    chunk_decay = jnp.exp(c * log_gamma)[None, :, None, None]

    def to_chunks(t):
        return t.reshape(b, n, c, h, dh).transpose(1, 0, 2, 3, 4)

    def step(state, inp):
        qc, kc, vc = inp
        scores = jnp.einsum('bihd,bjhd->bhij', qc, kc) * decay_mask[None]
        intra = jnp.einsum('bhij,bjhe->bihe', scores, vc)
        inter = jnp.einsum('bihd,bhde->bihe', qc, state) * query_decay
        new_state = state * chunk_decay + jnp.einsum('bjhd,bjhe->bhde', kc * key_decay, vc)
        return new_state, intra + inter

    state0 = jnp.zeros((b, h, dh, dh), jnp.float32)
    _, out = lax.scan(step, state0, (to_chunks(q), to_chunks(k), to_chunks(v)))
    return out.transpose(1, 0, 2, 3, 4).reshape(b, s, h, dh)


def hybrid_mixer(h, positions, w_in, sc_conv_w, cf_dw_w, cf_dw_b, cf_ln_w, cf_ln_b, ret_gn_w,
                 w_sc_out, w_cf_out, w_ret_out, w_mix_out):
    b, s, _ = h.shape
    (sc_b, sc_c, sc_h, cf_val, cf_gate, q, k, v, g,
     gate_sc, gate_cf, gate_ret) = jnp.split(h @ w_in, SPLIT_POINTS, axis=-1)

    y_sc = (sc_b * causal_depthwise_conv(sc_c * sc_h, sc_conv_w)) @ w_sc_out

    z = causal_depthwise_conv(cf_val * jax.nn.sigmoid(cf_gate), cf_dw_w) + cf_dw_b
    y_cf = jax.nn.silu(layer_norm(z, cf_ln_w, cf_ln_b)) @ w_cf_out

    log_gamma = jnp.log(1.0 - 2.0 ** (-5.0 - jnp.arange(RET_HEADS, dtype=jnp.float32)))
    qh = rotary(q.astype(jnp.float32).reshape(b, s, RET_HEADS, RET_HEAD_DIM), positions)
    kh = rotary(k.astype(jnp.float32).reshape(b, s, RET_HEADS, RET_HEAD_DIM), positions) * (RET_HEAD_DIM ** -0.5)
    vh = v.astype(jnp.float32).reshape(b, s, RET_HEADS, RET_HEAD_DIM)
    o = chunkwise_retention(qh, kh, vh, log_gamma)
    mu = jnp.mean(o, axis=-1, keepdims=True)
    var = jnp.mean(jnp.square(o - mu), axis=-1, keepdims=True)
    o = ((o - mu) * lax.rsqrt(var + EPS)).reshape(b, s, RET_WIDTH) * ret_gn_w.astype(jnp.float32)
    y_ret = (jax.nn.silu(g.astype(jnp.float32)) * o).astype(h.dtype) @ w_ret_out

    merged = (jax.nn.sigmoid(gate_sc) * y_sc + jax.nn.sigmoid(gate_cf) * y_cf
              + jax.nn.sigmoid(gate_ret) * y_ret)
    return merged @ w_mix_out


def setup_inputs(seed: int = 0) -> dict:
    key = jax.random.key(seed)
    ks = jax.random.split(key, 24)

    def dense(k, shape, fan_in):
        return jax.random.normal(k, shape, jnp.float32) * (fan_in ** -0.5)

    def gain(k, shape):
        return 1.0 + 0.02 * jax.random.normal(k, shape, jnp.float32)

    def bias(k, shape):
        return 0.02 * jax.random.normal(k, shape, jnp.float32)

    x = jax.random.normal(ks[0], (BATCH, SEQ, D_MODEL), jnp.float32)
    offsets = jax.random.randint(ks[1], (BATCH, 1), 0, 1024, dtype=jnp.int32)
    positions = offsets + jnp.arange(SEQ, dtype=jnp.int32)[None, :]
    return {
        'x': x,
        'positions': positions,
        'norm_ffn1': gain(ks[2], (DEPTH, D_MODEL)),
        'ffn1_in': dense(ks[3], (DEPTH, D_MODEL, 2 * D_FF), D_MODEL),
        'ffn1_out': dense(ks[4], (DEPTH, D_FF, D_MODEL), D_FF),
        'norm_mix': gain(ks[5], (DEPTH, D_MODEL)),
        'w_in': dense(ks[6], (DEPTH, D_MODEL, IN_WIDTH), D_MODEL),
        'sc_conv_w': dense(ks[7], (DEPTH, SC_KERNEL, SC_WIDTH), SC_KERNEL),
        'cf_dw_w': dense(ks[8], (DEPTH, CF_KERNEL, CF_WIDTH), CF_KERNEL),
        'cf_dw_b': bias(ks[9], (DEPTH, CF_WIDTH)),
        'cf_ln_w': gain(ks[10], (DEPTH, CF_WIDTH)),
        'cf_ln_b': bias(ks[11], (DEPTH, CF_WIDTH)),
        'ret_gn_w': gain(ks[12], (DEPTH, RET_WIDTH)),
        'w_sc_out': dense(ks[13], (DEPTH, SC_WIDTH, D_MODEL), SC_WIDTH),
        'w_cf_out': dense(ks[14], (DEPTH, CF_WIDTH, D_MODEL), CF_WIDTH),
        'w_ret_out': dense(ks[15], (DEPTH, RET_WIDTH, D_MODEL), RET_WIDTH),
        'w_mix_out': dense(ks[16], (DEPTH, D_MODEL, D_MODEL), D_MODEL),
        'norm_ffn2': gain(ks[17], (DEPTH, D_MODEL)),
        'ffn2_in': dense(ks[18], (DEPTH, D_MODEL, 2 * D_FF), D_MODEL),
        'ffn2_out': dense(ks[19], (DEPTH, D_FF, D_MODEL), D_FF),
        'norm_final': gain(ks[20], (D_MODEL,)),
    }


def reference(x, positions, norm_ffn1, ffn1_in, ffn1_out, norm_mix, w_in, sc_conv_w, cf_dw_w,
              cf_dw_b, cf_ln_w, cf_ln_b, ret_gn_w, w_sc_out, w_cf_out, w_ret_out, w_mix_out,
              norm_ffn2, ffn2_in, ffn2_out, norm_final):
    for l in range(DEPTH):
        x = x + 0.5 * swiglu(rms_norm(x, norm_ffn1[l]), ffn1_in[l], ffn1_out[l])
        x = x + hybrid_mixer(rms_norm(x, norm_mix[l]), positions, w_in[l], sc_conv_w[l], cf_dw_w[l],
                             cf_dw_b[l], cf_ln_w[l], cf_ln_b[l], ret_gn_w[l], w_sc_out[l],
                             w_cf_out[l], w_ret_out[l], w_mix_out[l])
        x = x + 0.5 * swiglu(rms_norm(x, norm_ffn2[l]), ffn2_in[l], ffn2_out[l])
    return rms_norm(x, norm_final)
```

```python
import numpy as np
import ml_dtypes
import concourse.bass as bass
import concourse.mybir as mybir
from concourse.bass_utils import run_bass_kernel_spmd

F32 = mybir.dt.float32
BF16 = mybir.dt.bfloat16
I32 = mybir.dt.int32
AF = mybir.ActivationFunctionType
ALU = mybir.AluOpType
AX = mybir.AxisListType

EPS = 1e-6
ROPE_BASE = 10000.0


def make_cfg(D=4096, SEQ=8192, BATCH=2, DEPTH=2, NCORES=8):
    c = dict(D=D, SEQ=SEQ, BATCH=BATCH, DEPTH=DEPTH, NCORES=NCORES)
    c["DC"] = D // 128
    c["FF"] = ((8 * D // 3 + 255) // 256) * 256
    c["FC"] = c["FF"] // 128
    c["SCW"] = D // 4
    c["CFW"] = D // 4
    c["RW"] = D // 2
    c["H"] = c["RW"] // 256
    c["SCC"] = c["SCW"] // 128
    c["CFC"] = c["CFW"] // 128
    c["RC"] = c["RW"] // 128
    c["CPB"] = NCORES // BATCH
    c["TOK"] = SEQ // c["CPB"]
    c["T"] = 512
    c["NT"] = c["TOK"] // c["T"]
    sizes = (c["SCW"],) * 3 + (c["CFW"],) * 2 + (c["RW"],) * 4 + (D,) * 3
    offs = np.concatenate([[0], np.cumsum(sizes)]).astype(int)
    c["IN_OFF"] = dict(zip(["sc_b", "sc_c", "sc_h", "cf_val", "cf_gate", "q", "k", "v", "g",
                            "gate_sc", "gate_cf", "gate_ret"], offs[:-1]))
    c["IN_W"] = int(offs[-1])
    ng = 4
    base, rem = divmod(c["FC"], ng)
    c["FGRP"] = [base + (1 if i < rem else 0) for i in range(ng)]
    c["EXW"] = c["RC"] * 256 + c["SCC"] * 2 + c["CFC"] * 30
    return c


ENGS = ("pe", "act", "dve", "pool", "sp")


class Op:
    __slots__ = ("eng", "fn", "deps", "dma", "semkey", "mark", "idx", "eidx", "need")

    def __init__(self, eng, fn, dma, semkey):
        self.eng = eng
        self.fn = fn
        self.deps = ()
        self.dma = dma
        self.semkey = semkey
        self.mark = None
        self.need = False


class Prog:
    def __init__(self):
        self.ops = []
        self.last_w = {}
        self.readers = {}
        self.ecount = {e: 0 for e in ENGS}
        self.last_dma_on_sem = {}

    R1FAM = frozenset(["xT", "act", "a_sc", "a_cf", "ogT", "mg", "z", "kd_tok", "v_tok", "g_tok", "og_tok",
                       "qT", "qdT", "kT", "kdT"])

    def add(self, eng, fn, reads=(), writes=(), dma=False, semkey=None):
        reads = list(reads)
        for k in list(reads) + list(writes):
            if isinstance(k, tuple) and k[0] in self.R1FAM:
                reads.append("R1sw")
                break
        op = Op(eng, fn, dma, semkey)
        op.idx = len(self.ops)
        op.eidx = self.ecount[eng]
        self.ecount[eng] += 1
        deps = set()
        for k in reads:
            w = self.last_w.get(k)
            if w is not None:
                deps.add(w)
        for k in writes:
            w = self.last_w.get(k)
            if w is not None:
                deps.add(w)
            r = self.readers.get(k)
            if r:
                deps.update(r)
        if dma:
            prev = self.last_dma_on_sem.get(semkey)
            if prev is not None:
                deps.add(prev)
            self.last_dma_on_sem[semkey] = op.idx
        op.deps = sorted(deps)
        for k in writes:
            self.last_w[k] = op.idx
            self.readers[k] = []
        for k in reads:
            self.readers.setdefault(k, []).append(op.idx)
        self.ops.append(op)
        return op

    def finalize(self):
        ops = self.ops
        for op in ops:
            for d in op.deps:
                p = ops[d]
                if p.dma or p.eng != op.eng:
                    p.need = True
                elif p.eng != "pe" and op.eidx - p.eidx < 3:
                    p.need = True
        cnt = {e: 0 for e in ENGS}
        dcnt = {}
        for op in ops:
            if op.dma:
                dcnt[op.semkey] = dcnt.get(op.semkey, 0) + 16
                op.mark = ("dma:%s" % (op.semkey,), dcnt[op.semkey])
            elif op.need:
                cnt[op.eng] += 1
                op.mark = ("eng:" + op.eng, cnt[op.eng])
        return ["eng:" + e for e in ENGS] + sorted({"dma:%s" % (k,) for k in dcnt})

    def emit_engine(self, eng, handle, sems):
        ops = self.ops
        seen = {}
        for op in ops:
            if op.eng != eng:
                continue
            for d in op.deps:
                p = ops[d]
                if p.mark is None:
                    continue
                if (not p.dma) and p.eng == eng and (eng == "pe" or op.eidx - p.eidx >= 3):
                    continue
                name, val = p.mark
                if seen.get(name, 0) >= val:
                    continue
                seen[name] = val
                handle.wait_ge(sems[name], val)
            ins = op.fn(handle)
            if op.mark is not None:
                name, val = op.mark
                ins.then_inc(sems[name], 16 if op.dma else 1)


def const_tables(cfg):
    H = cfg["H"]
    lg = np.log(np.float32(1.0) - np.float32(2.0) ** (-5.0 - np.arange(H, dtype=np.float32))).astype(np.float32)
    idx = np.arange(128, dtype=np.float32)
    rel = idx[None, :] - idx[:, None]
    maskT = np.where(rel[None] >= 0, np.exp(rel[None] * lg[:, None, None]), 0.0).astype(np.float32)
    qdec = np.exp((idx[None, :] + 1.0) * lg[:, None]).astype(np.float32)
    kdec = np.exp((127.0 - idx[None, :]) * lg[:, None]).astype(np.float32)
    cdec = np.exp(np.float32(128.0) * lg).astype(np.float32)
    half = 128
    inv_freq = (np.float32(ROPE_BASE) ** (-np.arange(half, dtype=np.float32) / np.float32(half))).astype(np.float32)
    tabs = np.zeros((128, 3 * H * 128 + 8), np.float32)
    tabs[:, 0:H * 128] = maskT.transpose(1, 0, 2).reshape(128, H * 128)
    tabs[:, H * 128:2 * H * 128] = np.broadcast_to(qdec.reshape(1, H * 128), (128, H * 128))
    tabs[:, 2 * H * 128:3 * H * 128] = np.broadcast_to((kdec * np.float32(0.0625)).reshape(1, H * 128), (128, H * 128))
    tabs[:, 3 * H * 128] = inv_freq
    return tabs, cdec, lg


class Builder:
    def __init__(self, cfg, segs, fused):
        self.cfg = cfg
        self.segs = segs
        self.fused = fused
        self.P = Prog()
        self.nc = bass.Bass("TRN2", target_bir_lowering=False)
        self.bank_rr = 0
        self.slot_rr = 0
        self.tmp_rr = 0
        self.xst_rr = 0
        self.uid = 0

    def din(self, name, shape, dt=F32):
        return self.nc.dram_tensor(name, list(shape), dt, kind="ExternalInput").ap()

    def dout(self, name, shape, dt=F32):
        return self.nc.dram_tensor(name, list(shape), dt, kind="ExternalOutput").ap()

    def dint(self, name, shape, dt=F32):
        return self.nc.dram_tensor(name, list(shape), dt, kind="Internal").ap()

    def bank(self):
        b = self.bank_rr
        self.bank_rr = (b + 1) % 8
        return b

    def tmp(self):
        i = self.tmp_rr
        self.tmp_rr = (i + 1) % self.NTMP
        return i

    def key(self, base):
        self.uid += 1
        return (base, self.uid)

    def dma(self, eng, out, in_, reads, writes, semkey):
        self.P.add(eng, lambda e: e.dma_start(out=out, in_=in_), reads=reads, writes=writes,
                   dma=True, semkey=semkey)

    def alloc(self):
        cfg, nc = self.cfg, self.nc
        DC, T, H = cfg["DC"], cfg["T"], cfg["H"]
        SCC, CFC, RC = cfg["SCC"], cfg["CFC"], cfg["RC"]
        self.hT = nc.alloc_sbuf_tensor("sb_hT", [128, DC, T], BF16).ap()
        maxg = max(cfg["FGRP"])
        n_ffn = DC * T + maxg * T // 2
        o = 0
        lay = {}
        for nm, words in (("a_sc", SCC * T // 2), ("a_cf", CFC * T // 2), ("ogT", RC * T // 2),
                          ("merged", DC * T // 2), ("ph", 7 * 512 + 1024)):
            lay[nm] = (o, words)
            o += words
        n_mix = o
        assert CFC * T <= DC * T // 2
        self.R1 = nc.alloc_sbuf_tensor("sb_R1", [128, max(n_ffn, n_mix)], F32).ap()
        R1 = self.R1
        self.xT = R1[:, 0:DC * T].rearrange("p (c t) -> p c t", t=T)
        self.act = R1[:, DC * T:DC * T + maxg * T // 2].bitcast(BF16).rearrange("p (c t) -> p c t", t=T)

        def bfv(nm, c):
            o, w = lay[nm]
            return R1[:, o:o + w].bitcast(BF16).rearrange("p (c t) -> p c t", c=c)
        self.a_sc = bfv("a_sc", SCC)
        self.a_cf = bfv("a_cf", CFC)
        self.ogT = bfv("ogT", RC)
        self.merged = bfv("merged", DC)
        o, w = lay["merged"]
        self.z = R1[:, o:o + CFC * T].rearrange("p (c t) -> p c t", t=T)
        o, w = lay["ph"]
        ph = R1[:, o:o + w]
        pb = ph[:, 0:7 * 512].bitcast(BF16)
        q = [0]

        def carve(n, t):
            v = pb[:, q[0]:q[0] + n].rearrange("p (c t) -> p c t", t=t)
            q[0] += n
            return v
        self.qT = carve(2 * T, T)
        self.qdT = carve(2 * T, T)
        self.kT = carve(2 * T, T)
        self.kdT = carve(2 * T, T)
        self.kd_tok = carve(4 * 256, 256)
        self.v_tok = carve(4 * 256, 256)
        self.og_tok = carve(4 * 256, 256)
        self.g_tok = ph[:, 7 * 512:7 * 512 + 1024].rearrange("p (c t) -> p c t", t=256)
        self.NSLOT = 2
        self.SLOTW = 8192
        self.ring = nc.alloc_sbuf_tensor("sb_ring", [128, self.NSLOT, self.SLOTW], BF16).ap()
        self.NTMP = 6
        self.tmpb = nc.alloc_sbuf_tensor("sb_tmpb", [128, self.NTMP, T], F32).ap()
        self.xst = nc.alloc_sbuf_tensor("sb_xst", [128, 2, T], F32).ap()
        self.rstd = nc.alloc_sbuf_tensor("sb_rstd", [128, 1, T], F32).ap()
        self.trig = nc.alloc_sbuf_tensor("sb_trig", [128, 2, T], F32).ap()
        self.uext = nc.alloc_sbuf_tensor("sb_uext", [128, T + 32], F32).ap()
        self.hal_sc = nc.alloc_sbuf_tensor("sb_hal_sc", [128, SCC, 2], F32).ap()
        self.hal_cf = nc.alloc_sbuf_tensor("sb_hal_cf", [128, CFC, 30], F32).ap()
        self.small = nc.alloc_sbuf_tensor("sb_small", [128, 16], F32).ap()
        self.ones = nc.alloc_sbuf_tensor("sb_ones", [128, 128], F32).ap()
        self.ident = nc.alloc_sbuf_tensor("sb_ident", [128, 128], BF16).ap()
        self.tabs = nc.alloc_sbuf_tensor("sb_tabs", [128, 3 * H * 128 + 8], F32).ap()
        self.gnw = nc.alloc_sbuf_tensor("sb_gnw", [128, 2, 256], F32).ap()
        L = cfg["DEPTH"]
        self.NPV = 3 * DC + 3 * SCC + 34 * CFC
        self.pv = nc.alloc_sbuf_tensor("sb_pv", [128, L * self.NPV + DC], F32).ap()
        self.coef = nc.alloc_sbuf_tensor("sb_coef", [128, 8 * H + 8], F32).ap()
        self.Sh = nc.alloc_sbuf_tensor("sb_Sh", [128, 2, 2, 256], F32).ap()
        self.Shb = nc.alloc_sbuf_tensor("sb_Shb", [128, 2, 2, 256], BF16).ap()
        self.sh_rr = 0
        self.ps = [nc.alloc_psum_tensor("ps%d" % i, [128, 512], F32).ap() for i in range(8)]

    def declare(self):
        cfg = self.cfg
        D, DC, FC, TOK, H = cfg["D"], cfg["DC"], cfg["FC"], cfg["TOK"], cfg["H"]
        SCC, CFC, RC = cfg["SCC"], cfg["CFC"], cfg["RC"]
        first = self.segs[0] == ("A", 0)
        last = self.segs[-1] == ("B", cfg["DEPTH"] - 1)
        self.first, self.last = first, last
        layers = sorted({l for _, l in self.segs})
        self.W = {}
        for l in layers:
            w = {}
            kinds = {k for k, ll in self.segs if ll == l}
            if "A" in kinds:
                w["f1i"] = self.din("f1i_%d" % l, [2 * FC, 128, DC * 128])
                w["f1o"] = self.din("f1o_%d" % l, [DC, 128, FC * 128])
            w["win"] = self.din("win_%d" % l, [cfg["IN_W"] // 128, 128, DC * 128])
            w["wvg"] = self.din("wvg_%d" % l, [2 * H, 128, DC * 256])
            if "B" in kinds:
                w["sco"] = self.din("sco_%d" % l, [DC, 128, SCC * 128])
                w["cfo"] = self.din("cfo_%d" % l, [DC, 128, CFC * 128])
                w["reo"] = self.din("reo_%d" % l, [DC, 128, RC * 128])
                w["mxo"] = self.din("mxo_%d" % l, [DC, 128, DC * 128])
                w["f2i"] = self.din("f2i_%d" % l, [2 * FC, 128, DC * 128])
                w["f2o"] = self.din("f2o_%d" % l, [DC, 128, FC * 128])
            self.W[l] = w
        self.d_pv = self.din("pv", [128, cfg["DEPTH"] * self.NPV + DC])
        self.d_gnw = self.din("gnw", [cfg["DEPTH"], RC * 128])
        self.d_tabs = self.din("tabs", [128, 3 * H * 128 + 8])
        self.d_coef = self.din("coef", [128, 8 * H + 8])
        self.d_pos = self.din("pos", [1, TOK], I32)
        self.d_ident = self.din("ident", [128, 128], F32)
        if first:
            self.d_xin = self.din("xin", [D, TOK])
        if self.fused:
            self.d_xs = self.dint("xs", [D, TOK])
        else:
            if not first:
                self.d_xsin = self.din("xsin", [D, TOK])
            if not last:
                self.d_xs = self.dout("xs", [D, TOK])
            else:
                self.d_xs = self.dint("xs", [D, TOK])
        if last:
            self.d_out = self.dout("out", [D, TOK])
        EXW = cfg["EXW"]
        if not self.fused:
            if any(k == "A" for k, _ in self.segs):
                self.d_exo = self.dout("exo", [128, EXW])
            if any(k == "B" for k, _ in self.segs):
                self.d_exi = self.din("exi", [4, 128, EXW])
        else:
            self.d_exo = [self.dint("exo%d" % l, [128, EXW]) for l in range(cfg["DEPTH"])]
            self.d_exi = [self.dint("exi%d" % l, [4 * 128, EXW]) for l in range(cfg["DEPTH"])]

    def prologue(self):
        cfg, P = self.cfg, self.P
        self.dma("sp", self.pv, self.d_pv, [], ["pv"], "pv")
        self.dma("sp", self.tabs, self.d_tabs, [], ["tabs"], "tabs")
        self.dma("sp", self.coef, self.d_coef, [], ["coef"], "coef")
        idf = self.tmpb[:, 0, 0:128]
        self.dma("sp", idf, self.d_ident, [], [("tmp", 0)], "ident")
        P.add("dve", lambda e: e.tensor_copy(out=self.ident, in_=idf), reads=[("tmp", 0)], writes=["ident"])
        P.add("dve", lambda e: e.memset(self.ones, 1.0), writes=["ones"])
        if not self.fused and not self.first:
            self.dma("sp", self.d_xs, self.d_xsin, [], ["xs_all"], "xscopy")

    def pvcol(self, l, name, c, n=1):
        cfg = self.cfg
        DC, SCC, CFC = cfg["DC"], cfg["SCC"], cfg["CFC"]
        offs = {"n1": 0, "nm": DC, "n2": 2 * DC, "scw": 3 * DC, "cfw": 3 * DC + 3 * SCC,
                "cfb": 3 * DC + 3 * SCC + 31 * CFC, "lnw": 3 * DC + 3 * SCC + 32 * CFC,
                "lnb": 3 * DC + 3 * SCC + 33 * CFC}
        if name == "nf":
            b = cfg["DEPTH"] * self.NPV + c
        else:
            b = l * self.NPV + offs[name] + c
        return self.pv[:, b:b + n]

    def trig_tile(self, t):
        cfg, P = self.cfg, self.P
        T, H = cfg["T"], cfg["H"]
        invf = self.tabs[:, 3 * H * 128:3 * H * 128 + 1]
        TWO_PI = 2.0 * np.pi
        C1 = 6.28125
        C2 = TWO_PI - C1
        ta, tb, tc, tp = self.tmp(), self.tmp(), self.tmp(), self.tmp()
        A, Bk, Ck = self.tmpb[:, ta, :], self.tmpb[:, tb, :], self.tmpb[:, tc, :]
        pos = self.tmpb[:, tp, :]
        posi = self.tmpb[:, tc, :].bitcast(I32)
        self.dma("sp", posi, self.d_pos[0:1, t * T:(t + 1) * T].partition_broadcast(128), [], [("tmp", tc)], "posi")
        P.add("dve", lambda e: e.tensor_copy(out=pos, in_=posi), reads=[("tmp", tc)], writes=[("tmp", tp)])
        MAGIC = 12582912.0
        for which, shift in ((1, 0.0), (0, np.pi / 2)):
            P.add("dve", lambda e, s=shift: e.tensor_scalar(out=A, in0=pos, scalar1=invf, scalar2=float(s),
                                                           op0=ALU.mult, op1=ALU.add),
                  reads=[("tmp", tp), "tabs"], writes=[("tmp", ta)])
            P.add("dve", lambda e: e.tensor_scalar(out=Bk, in0=A, scalar1=float(1.0 / TWO_PI), scalar2=MAGIC,
                                                   op0=ALU.mult, op1=ALU.add),
                  reads=[("tmp", ta)], writes=[("tmp", tb)])
            P.add("dve", lambda e: e.tensor_scalar(out=Ck, in0=Bk, scalar1=-MAGIC, scalar2=None, op0=ALU.add),
                  reads=[("tmp", tb)], writes=[("tmp", tc)])
            P.add("dve", lambda e: e.scalar_tensor_tensor(out=Bk, in0=Ck, scalar=-C1, in1=A, op0=ALU.mult, op1=ALU.add),
                  reads=[("tmp", tc), ("tmp", ta)], writes=[("tmp", tb)])
            P.add("dve", lambda e: e.scalar_tensor_tensor(out=A, in0=Ck, scalar=-C2, in1=Bk, op0=ALU.mult, op1=ALU.add),
                  reads=[("tmp", tc), ("tmp", tb)], writes=[("tmp", ta)])
            P.add("dve", lambda e: e.tensor_scalar(out=A, in0=A, scalar1=3.14159, scalar2=-3.14159,
                                                   op0=ALU.min, op1=ALU.max),
                  reads=[("tmp", ta)], writes=[("tmp", ta)])
            dst = self.trig[:, which, :]
            P.add("act", lambda e, dst=dst: e.activation(out=dst, in_=A, func=AF.Sin),
                  reads=[("tmp", ta)], writes=[("trig", which)])

    def load_slab(self, parts):
        s = self.slot_rr
        self.slot_rr = (s + 1) % self.NSLOT
        off = 0
        offs = []
        for i, (src, n) in enumerate(parts):
            dst = self.ring[:, s, off:off + n]
            keys = [("w", s, q) for q in range(off // 1024, (off + n - 1) // 1024 + 1)]
            self.dma("pool", dst, src, [], keys, ("w", s, i))
            offs.append(off)
            off += n
        assert off <= self.SLOTW
        return s, offs

    def wkeys(self, s, off, n):
        return [("w", s, q) for q in range(off // 1024, (off + n - 1) // 1024 + 1)]

    def lin_fm(self, slot, woff, nk, rhs_fn, rhs_keys, ncol=None):
        b = self.bank()
        T = self.cfg["T"] if ncol is None else ncol
        out = self.ps[b][:, 0:T]
        ring = self.ring

        def fn(e):
            ins = None
            for k in range(nk):
                ins = e.matmul(out, lhsT=ring[:, slot, woff + k * 128:woff + (k + 1) * 128], rhs=rhs_fn(k),
                               start=(k == 0), stop=(k == nk - 1))
            return ins
        self.P.add("pe", fn, reads=self.wkeys(slot, woff, nk * 128) + list(rhs_keys), writes=[("ps", b)])
        return b

    def rms_stats(self, xfn, D, which=0):
        cfg, P = self.cfg, self.P
        DC, T = cfg["DC"], cfg["T"]
        b = self.bank()
        for d in range(DC):
            xa, xk = xfn(d)
            ti = self.tmp()
            sq = self.tmpb[:, ti, :]
            P.add("act", lambda e, xa=xa, sq=sq: e.activation(out=sq, in_=xa, func=AF.Square),
                  reads=xk, writes=[("tmp", ti)])
            P.add("pe", lambda e, sq=sq, d=d: e.matmul(self.ps[b], lhsT=self.ones, rhs=sq, start=(d == 0),
                                                       stop=(d == DC - 1)),
                  reads=[("tmp", ti), "ones"], writes=[("ps", b)])
        rs = self.rstd[:, 0, :]
        P.add("act", lambda e: e.activation(out=rs, in_=self.ps[b], func=AF.Sqrt, bias=EPS, scale=1.0 / D),
              reads=[("ps", b)], writes=[("rstd", 0)])
        P.add("dve", lambda e: e.reciprocal(out=rs, in_=rs), reads=[("rstd", 0)], writes=[("rstd", 0)])

    def norm_resident(self, l, gname):
        cfg, P = self.cfg, self.P
        DC = cfg["DC"]
        self.rms_stats(lambda d: (self.xT[:, d, :], [("xT", d)]), cfg["D"])
        for d in range(DC):
            g = self.pvcol(l, gname, d)
            P.add("dve", lambda e, d=d, g=g: e.scalar_tensor_tensor(out=self.hT[:, d, :], in0=self.xT[:, d, :], scalar=g,
                                                                   in1=self.rstd[:, 0, :], op0=ALU.mult, op1=ALU.mult),
                  reads=[("xT", d), ("rstd", 0), "pv"], writes=[("hT", d)])

    def x_stream(self, t, d):
        T = self.cfg["T"]
        s = self.xst_rr
        self.xst_rr = 1 - s
        src = self.d_xs[d * 128:(d + 1) * 128, t * T:(t + 1) * T]
        self.dma("sp", self.xst[:, s, :], src, [("xs", t, d), "xs_all"], [("xst", s)], ("xst", s))
        return self.xst[:, s, :], [("xst", s)], s

    def norm_stream(self, l, gname, t):
        cfg, P = self.cfg, self.P
        DC = cfg["DC"]
        self.rms_stats(lambda d: self.x_stream(t, d)[0:2], cfg["D"])
        for d in range(DC):
            xa, xk, s = self.x_stream(t, d)
            g = self.pvcol(l, gname, d)
            P.add("dve", lambda e, d=d, g=g, xa=xa: e.scalar_tensor_tensor(out=self.hT[:, d, :], in0=xa, scalar=g,
                                                                          in1=self.rstd[:, 0, :], op0=ALU.mult, op1=ALU.mult),
                  reads=xk + [("rstd", 0), "pv"], writes=[("hT", d)])

    def ffn(self, l, wi, wo):
        cfg, P = self.cfg, self.P
        DC, FC, T = cfg["DC"], cfg["FC"], cfg["T"]
        KW = DC * 128
        rk = [("hT", d) for d in range(DC)]
        j0 = 0
        for gsz in cfg["FGRP"]:
            for jj in range(gsz):
                j = j0 + jj
                s, (og_, ou_) = self.load_slab([(wi[j], KW), (wi[FC + j], KW)])
                bg = self.lin_fm(s, og_, DC, lambda k: self.hT[:, k, :], rk)
                bu = self.lin_fm(s, ou_, DC, lambda k: self.hT[:, k, :], rk)
                ti = self.tmp()
                sg = self.tmpb[:, ti, :]
                P.add("act", lambda e, sg=sg, bg=bg: e.activation(out=sg, in_=self.ps[bg], func=AF.Silu),
                      reads=[("ps", bg)], writes=[("tmp", ti)])
                P.add("dve", lambda e, sg=sg, bu=bu, jj=jj: e.tensor_tensor(out=self.act[:, jj, :], in0=sg, in1=self.ps[bu],
                                                                           op=ALU.mult),
                      reads=[("tmp", ti), ("ps", bu)], writes=[("act", jj)])
            for d in range(DC):
                s, (o0,) = self.load_slab([(wo[d][:, j0 * 128:(j0 + gsz) * 128], gsz * 128)])
                b = self.lin_fm(s, o0, gsz, lambda k: self.act[:, k, :], [("act", k) for k in range(gsz)])
                P.add("dve", lambda e, d=d, b=b: e.scalar_tensor_tensor(out=self.xT[:, d, :], in0=self.ps[b], scalar=0.5,
                                                                       in1=self.xT[:, d, :], op0=ALU.mult, op1=ALU.add),
                      reads=[("ps", b), ("xT", d)], writes=[("xT", d)])
            j0 += gsz

    def load_xT(self, src, t, keys):
        cfg = self.cfg
        DC, T = cfg["DC"], cfg["T"]
        v = src.rearrange("(c p) n -> p c n", p=128)[:, :, t * T:(t + 1) * T]
        self.dma("sp", self.xT, v, keys, [("xT", d) for d in range(DC)], "xT")

    def store_xT(self, t):
        cfg = self.cfg
        DC, T = cfg["DC"], cfg["T"]
        v = self.d_xs.rearrange("(c p) n -> p c n", p=128)[:, :, t * T:(t + 1) * T]
        self.dma("sp", v, self.xT, [("xT", d) for d in range(DC)], [("xs", t, d) for d in range(DC)], "xTst")

    def rotary(self, b0, b1, outT, outdT, dec, scale, outT_key, outdT_key):
        cfg, P = self.cfg, self.P
        T = cfg["T"]
        cos = self.trig[:, 0, :]
        sin = self.trig[:, 1, :]
        tk = [("trig", 0), ("trig", 1)]
        ia, ib = self.tmp(), self.tmp()
        A, Bt = self.tmpb[:, ia, :], self.tmpb[:, ib, :]
        decb = dec.unsqueeze(1).to_broadcast([128, T // 128, 128])
        for c, (pa, pb_, opn) in enumerate(((b0, b1, ALU.subtract), (b1, b0, ALU.add))):
            P.add("dve", lambda e, pa=pa: e.tensor_tensor(out=A, in0=self.ps[pa], in1=cos, op=ALU.mult),
                  reads=[("ps", pa)] + tk, writes=[("tmp", ia)])
            P.add("dve", lambda e, pb_=pb_: e.tensor_tensor(out=Bt, in0=self.ps[pb_], in1=sin, op=ALU.mult),
                  reads=[("ps", pb_)] + tk, writes=[("tmp", ib)])
            P.add("dve", lambda e, opn=opn: e.tensor_tensor(out=A, in0=A, in1=Bt, op=opn),
                  reads=[("tmp", ia), ("tmp", ib)], writes=[("tmp", ia)])
            if outT is not None:
                P.add("act", lambda e, c=c: e.activation(out=outT[:, c, :], in_=A, func=AF.Copy, scale=float(scale)),
                      reads=[("tmp", ia)], writes=[(outT_key, c)])
            P.add("dve", lambda e, c=c: e.tensor_tensor(out=outdT[:, c, :].rearrange("p (a n) -> p a n", n=128),
                                                         in0=A.rearrange("p (a n) -> p a n", n=128), in1=decb, op=ALU.mult),
                  reads=[("tmp", ia), "tabs"], writes=[(outdT_key, c)])

    def tok_linear(self, slab_src, post):
        cfg, P = self.cfg, self.P
        DC = cfg["DC"]
        s, (o0,) = self.load_slab([(slab_src, DC * 256)])
        for tb in range(4):
            b = self.bank()
            out = self.ps[b][:, 0:256]

            def fn(e, tb=tb, out=out):
                ins = None
                for k in range(DC):
                    ins = e.matmul(out, lhsT=self.hT[:, k, tb * 128:(tb + 1) * 128],
                                   rhs=self.ring[:, s, k * 256:(k + 1) * 256], start=(k == 0), stop=(k == DC - 1))
                return ins
            P.add("pe", fn, reads=self.wkeys(s, 0, DC * 256) + [("hT", d) for d in range(DC)], writes=[("ps", b)])
            post(tb, b)

    def ret_head(self, l, h, mode, first_tile):
        cfg, P = self.cfg, self.P
        DC, T, H = cfg["DC"], cfg["T"], cfg["H"]
        W = self.W[l]
        KW = DC * 128
        rk = [("hT", d) for d in range(DC)]
        qc0 = cfg["IN_OFF"]["q"] // 128 + 2 * h
        kc0 = cfg["IN_OFF"]["k"] // 128 + 2 * h
        HT = H * 128
        maskT = self.tabs[:, h * 128:(h + 1) * 128]
        qdec = self.tabs[:, HT + h * 128:HT + (h + 1) * 128]
        kdec = self.tabs[:, 2 * HT + h * 128:2 * HT + (h + 1) * 128]
        cd = float(self.cdec[h])
        si = self.sh_rr
        self.sh_rr = 1 - si
        Sh, Shb = self.Sh[:, si], self.Shb[:, si]
        skeys = [("Sh", si, 0), ("Sh", si, 1)]
        park = self.d_park[:, 2 * h * 256:(2 * h + 2) * 256]
        if mode == "A" and first_tile:
            for c in range(2):
                P.add("dve", lambda e, c=c: e.memset(Sh[:, c, :], 0.0), writes=[("Sh", si, c)])
        else:
            self.dma("sp", Sh.rearrange("p c t -> p (c t)"), park, [("park", h)], skeys, ("Shl", si))
        if mode == "B":
            gi = si
            self.dma("sp", self.gnw[:, gi, :], self.d_gnw[l:l + 1, h * 256:(h + 1) * 256].partition_broadcast(128), [],
                     [("gnw", gi)], ("gnw", gi))
            for c in range(2):
                P.add("act", lambda e, c=c: e.copy(out=Shb[:, c, :], in_=Sh[:, c, :]), reads=[("Sh", si, c)],
                      writes=[("Shb", si, c)])
            s, (o0, o1) = self.load_slab([(W["win"][qc0], KW), (W["win"][qc0 + 1], KW)])
            b0 = self.lin_fm(s, o0, DC, lambda k: self.hT[:, k, :], rk)
            b1 = self.lin_fm(s, o1, DC, lambda k: self.hT[:, k, :], rk)
            self.rotary(b0, b1, self.qT, self.qdT, qdec, 1.0, "qT", "qdT")
        s, (o0, o1) = self.load_slab([(W["win"][kc0], KW), (W["win"][kc0 + 1], KW)])
        b0 = self.lin_fm(s, o0, DC, lambda k: self.hT[:, k, :], rk)
        b1 = self.lin_fm(s, o1, DC, lambda k: self.hT[:, k, :], rk)
        self.rotary(b0, b1, self.kT if mode == "B" else None, self.kdT, kdec, 0.0625, "kT", "kdT")
        for tb in range(4):
            b = self.bank()
            pb = self.ps[b].bitcast(BF16)
            for c in range(2):
                P.add("pe", lambda e, c=c, tb=tb, pb=pb: e.transpose(pb[:, c * 128:(c + 1) * 128],
                                                                     self.kdT[:, c, tb * 128:(tb + 1) * 128], self.ident),
                      reads=[("kdT", c), "ident"], writes=[("ps", b)])
            P.add("act", lambda e, tb=tb, pb=pb: e.copy(out=self.kd_tok[:, tb, :], in_=pb[:, 0:256]),
                  reads=[("ps", b)], writes=[("kd_tok", tb)])

        def post_v(tb, b):
            P.add("act", lambda e: e.copy(out=self.v_tok[:, tb, :], in_=self.ps[b][:, 0:256]),
                  reads=[("ps", b)], writes=[("v_tok", tb)])
        self.tok_linear(W["wvg"][h], post_v)
        if mode == "B":
            def post_g(tb, b):
                P.add("act", lambda e: e.activation(out=self.g_tok[:, tb, :], in_=self.ps[b][:, 0:256], func=AF.Silu),
                      reads=[("ps", b)], writes=[("g_tok", tb)])
            self.tok_linear(W["wvg"][H + h], post_g)
        for tb in range(4):
            tsl = slice(tb * 128, (tb + 1) * 128)
            if mode == "B":
                bs = self.bank()

                def fn_s(e, bs=bs, tsl=tsl):
                    ins = None
                    for c in range(2):
                        ins = e.matmul(self.ps[bs][:, 0:128], lhsT=self.kT[:, c, tsl], rhs=self.qT[:, c, tsl],
                                       start=(c == 0), stop=(c == 1))
                    return ins
                P.add("pe", fn_s, reads=[("kT", 0), ("kT", 1), ("qT", 0), ("qT", 1)],
                      writes=[("ps", bs)])
                ti = self.tmp()
                sm = self.tmpb[:, ti, 0:64].bitcast(BF16)
                P.add("dve", lambda e, bs=bs, sm=sm: e.tensor_tensor(out=sm, in0=self.ps[bs][:, 0:128], in1=maskT, op=ALU.mult),
                      reads=[("ps", bs), "tabs"], writes=[("tmp", ti)])
                bo = self.bank()

                def fn_o(e, bo=bo, sm=sm, tb=tb, tsl=tsl):
                    o = self.ps[bo][:, 0:256]
                    e.matmul(o, lhsT=sm, rhs=self.v_tok[:, tb, :], start=True, stop=False)
                    e.matmul(o, lhsT=self.qdT[:, 0, tsl], rhs=Shb[:, 0, :], start=False, stop=False)
                    return e.matmul(o, lhsT=self.qdT[:, 1, tsl], rhs=Shb[:, 1, :], start=False, stop=True)
                P.add("pe", fn_o, reads=[("tmp", ti), ("v_tok", tb), ("qdT", 0), ("qdT", 1),
                                         ("Shb", si, 0), ("Shb", si, 1)], writes=[("ps", bo)])
                self.groupnorm_gate(l, h, tb, bo, si)
            for c in range(2):
                b = self.bank()
                P.add("pe", lambda e, b=b, c=c, tb=tb: e.matmul(self.ps[b][:, 0:256], lhsT=self.kd_tok[:, tb, c * 128:(c + 1) * 128],
                                                                rhs=self.v_tok[:, tb, :], start=True, stop=True),
                      reads=[("kd_tok", tb), ("v_tok", tb)], writes=[("ps", b)])
                P.add("dve", lambda e, b=b, c=c: e.scalar_tensor_tensor(out=Sh[:, c, :], in0=Sh[:, c, :], scalar=cd,
                                                                       in1=self.ps[b][:, 0:256], op0=ALU.mult, op1=ALU.add),
                      reads=[("ps", b), ("Sh", si, c)], writes=[("Sh", si, c)])
                if mode == "B" and tb < 3:
                    P.add("act", lambda e, c=c: e.copy(out=Shb[:, c, :], in_=Sh[:, c, :]),
                          reads=[("Sh", si, c)], writes=[("Shb", si, c)])
        self.dma("sp", park, Sh.rearrange("p c t -> p (c t)"), skeys, [("park", h)], ("Shs", si))

    def groupnorm_gate(self, l, h, tb, bo, gi):
        cfg, P = self.cfg, self.P
        o = self.ps[bo][:, 0:256]
        sm = self.small
        ti = self.tmp()
        cen = self.tmpb[:, ti, 0:256]
        sq = self.tmpb[:, ti, 256:512]
        P.add("dve", lambda e: e.reduce_sum(out=sm[:, 0:1], in_=o, axis=AX.X), reads=[("ps", bo)], writes=[("sm", 0)])
        P.add("dve", lambda e: e.tensor_scalar(out=sm[:, 1:2], in0=sm[:, 0:1], scalar1=1.0 / 256, scalar2=None, op0=ALU.mult),
              reads=[("sm", 0)], writes=[("sm", 1)])
        P.add("dve", lambda e: e.tensor_scalar(out=cen, in0=o, scalar1=sm[:, 1:2], scalar2=None, op0=ALU.subtract),
              reads=[("ps", bo), ("sm", 1)], writes=[("tmp", ti)])
        P.add("act", lambda e: e.activation(out=sq, in_=cen, func=AF.Square, accum_out=sm[:, 2:3]),
              reads=[("tmp", ti)], writes=[("tmpsq", ti), ("sm", 2)])
        P.add("act", lambda e: e.activation(out=sm[:, 3:4], in_=sm[:, 2:3], func=AF.Sqrt, bias=EPS, scale=1.0 / 256),
              reads=[("sm", 2)], writes=[("sm", 3)])
        P.add("dve", lambda e: e.reciprocal(out=sm[:, 4:5], in_=sm[:, 3:4]), reads=[("sm", 3)], writes=[("sm", 4)])
        gn = self.gnw[:, gi, :]
        P.add("dve", lambda e: e.scalar_tensor_tensor(out=cen, in0=cen, scalar=sm[:, 4:5], in1=gn, op0=ALU.mult, op1=ALU.mult),
              reads=[("tmp", ti), ("sm", 4), ("gnw", gi), ("tmpsq", ti)], writes=[("tmp", ti)])
        P.add("dve", lambda e: e.tensor_tensor(out=self.og_tok[:, tb, :], in0=cen, in1=self.g_tok[:, tb, :], op=ALU.mult),
              reads=[("tmp", ti), ("g_tok", tb)], writes=[("og_tok", tb)])
        b = self.bank()
        pb = self.ps[b].bitcast(BF16)
        for c in range(2):
            P.add("pe", lambda e, c=c: e.transpose(pb[:, c * 128:(c + 1) * 128], self.og_tok[:, tb, c * 128:(c + 1) * 128],
                                                   self.ident),
                  reads=[("og_tok", tb), "ident"], writes=[("ps", b)])
        for c in range(2):
            P.add("act", lambda e, c=c: e.copy(out=self.ogT[:, 2 * h + c, tb * 128:(tb + 1) * 128],
                                               in_=pb[:, c * 128:(c + 1) * 128]),
                  reads=[("ps", b)], writes=[("ogT", 2 * h + c)])

    def sc_branch(self, l, t):
        cfg, P = self.cfg, self.P
        DC, T, SCC = cfg["DC"], cfg["T"], cfg["SCC"]
        W = self.W[l]
        KW = DC * 128
        rk = [("hT", d) for d in range(DC)]
        ob, oc, oh = (cfg["IN_OFF"][n] // 128 for n in ("sc_b", "sc_c", "sc_h"))
        ue = self.uext
        hfn = lambda k: self.hT[:, k, :]
        for c in range(SCC):
            s, (o0, o1) = self.load_slab([(W["win"][oc + c], KW), (W["win"][oh + c], KW)])
            bc = self.lin_fm(s, o0, DC, hfn, rk)
            bh = self.lin_fm(s, o1, DC, hfn, rk)
            s2, (o2,) = self.load_slab([(W["win"][ob + c], KW)])
            bb = self.lin_fm(s2, o2, DC, hfn, rk)
            ti, tj = self.tmp(), self.tmp()
            A, Z = self.tmpb[:, ti, :], self.tmpb[:, tj, :]
            P.add("act", lambda e, bc=bc, A=A: e.copy(out=A, in_=self.ps[bc]), reads=[("ps", bc)], writes=[("tmp", ti)])
            P.add("dve", lambda e, c=c: e.tensor_copy(out=ue[:, 0:2], in_=self.hal_sc[:, c, :]),
                  reads=[("hal_sc", c)], writes=["ue_h"])
            P.add("dve", lambda e, bh=bh, A=A: e.tensor_tensor(out=ue[:, 2:2 + T], in0=A, in1=self.ps[bh], op=ALU.mult),
                  reads=[("tmp", ti), ("ps", bh)], writes=["ue"])
            P.add("dve", lambda e, c=c: e.tensor_copy(out=self.hal_sc[:, c, :], in_=ue[:, T:T + 2]),
                  reads=["ue"], writes=[("hal_sc", c)])
            w = self.pvcol(l, "scw", 3 * c, 3)
            P.add("dve", lambda e, w=w, Z=Z: e.tensor_scalar(out=Z, in0=ue[:, 0:T], scalar1=w[:, 0:1], scalar2=None, op0=ALU.mult),
                  reads=["ue", "ue_h", "pv"], writes=[("tmp", tj)])
            for j in (1, 2):
                P.add("dve", lambda e, w=w, Z=Z, j=j: e.scalar_tensor_tensor(out=Z, in0=ue[:, j:j + T], scalar=w[:, j:j + 1],
                                                                            in1=Z, op0=ALU.mult, op1=ALU.add),
                      reads=["ue", "ue_h", ("tmp", tj)], writes=[("tmp", tj)])
            P.add("dve", lambda e, c=c, Z=Z, bb=bb: e.tensor_tensor(out=self.a_sc[:, c, :], in0=Z, in1=self.ps[bb], op=ALU.mult),
                  reads=[("tmp", tj), ("ps", bb)], writes=[("a_sc", c)])

    def cf_u(self, l, c, ncol, rhs_fn):
        cfg, P = self.cfg, self.P
        DC = cfg["DC"]
        W = self.W[l]
        KW = DC * 128
        rk = [("hT", d) for d in range(DC)]
        ov, og = (cfg["IN_OFF"][n] // 128 for n in ("cf_val", "cf_gate"))
        s, (o0, o1) = self.load_slab([(W["win"][ov + c], KW), (W["win"][og + c], KW)])
        bv = self.lin_fm(s, o0, DC, rhs_fn, rk, ncol=ncol)
        bg = self.lin_fm(s, o1, DC, rhs_fn, rk, ncol=ncol)
        ti = self.tmp()
        A = self.tmpb[:, ti, 0:ncol]
        P.add("act", lambda e: e.activation(out=A, in_=self.ps[bg][:, 0:ncol], func=AF.Sigmoid),
              reads=[("ps", bg)], writes=[("tmp", ti)])
        return bv, ti

    def cf_branch(self, l, t):
        cfg, P = self.cfg, self.P
        DC, T, CFC = cfg["DC"], cfg["T"], cfg["CFC"]
        ue = self.uext
        for c in range(CFC):
            bv, ti = self.cf_u(l, c, T, lambda k: self.hT[:, k, :])
            A = self.tmpb[:, ti, :]
            P.add("dve", lambda e, c=c: e.tensor_copy(out=ue[:, 0:30], in_=self.hal_cf[:, c, :]),
                  reads=[("hal_cf", c)], writes=["ue_h"])
            P.add("dve", lambda e, bv=bv, A=A: e.tensor_tensor(out=ue[:, 30:30 + T], in0=A, in1=self.ps[bv], op=ALU.mult),
                  reads=[("tmp", ti), ("ps", bv)], writes=["ue"])
            P.add("dve", lambda e, c=c: e.tensor_copy(out=self.hal_cf[:, c, :], in_=ue[:, T:T + 30]),
                  reads=["ue"], writes=[("hal_cf", c)])
            w = self.pvcol(l, "cfw", 31 * c, 31)
            bcol = self.pvcol(l, "cfb", c)
            Z = self.z[:, c, :]
            P.add("dve", lambda e, w=w, Z=Z, bcol=bcol: e.tensor_scalar(out=Z, in0=ue[:, 0:T], scalar1=w[:, 0:1], scalar2=bcol,
                                                                       op0=ALU.mult, op1=ALU.add),
                  reads=["ue", "ue_h", "pv"], writes=[("z", c)])
            for j in range(1, 31):
                P.add("dve", lambda e, w=w, Z=Z, j=j: e.scalar_tensor_tensor(out=Z, in0=ue[:, j:j + T], scalar=w[:, j:j + 1],
                                                                            in1=Z, op0=ALU.mult, op1=ALU.add),
                      reads=["ue", "ue_h", ("z", c)], writes=[("z", c)])
        b1, b2 = self.bank(), self.bank()
        for c in range(CFC):
            ti = self.tmp()
            sq = self.tmpb[:, ti, :]
            P.add("act", lambda e, c=c, sq=sq: e.activation(out=sq, in_=self.z[:, c, :], func=AF.Square),
                  reads=[("z", c)], writes=[("tmp", ti)])
            P.add("pe", lambda e, c=c: e.matmul(self.ps[b1], lhsT=self.ones, rhs=self.z[:, c, :], start=(c == 0), stop=(c == CFC - 1)),
                  reads=[("z", c), "ones"], writes=[("ps", b1)])
            P.add("pe", lambda e, c=c, sq=sq: e.matmul(self.ps[b2], lhsT=self.ones, rhs=sq, start=(c == 0), stop=(c == CFC - 1)),
                  reads=[("tmp", ti), "ones"], writes=[("ps", b2)])
        im, iv = self.tmp(), self.tmp()
        mean, var = self.tmpb[:, im, :], self.tmpb[:, iv, :]
        CW = cfg["CFW"]
        P.add("dve", lambda e: e.tensor_scalar(out=mean, in0=self.ps[b1], scalar1=1.0 / CW, scalar2=None, op0=ALU.mult),
              reads=[("ps", b1)], writes=[("tmp", im)])
        P.add("dve", lambda e: e.tensor_tensor(out=var, in0=mean, in1=mean, op=ALU.mult),
              reads=[("tmp", im)], writes=[("tmp", iv)])
        P.add("dve", lambda e: e.scalar_tensor_tensor(out=var, in0=self.ps[b2], scalar=1.0 / CW, in1=var, op0=ALU.mult, op1=ALU.subtract),
              reads=[("ps", b2), ("tmp", iv)], writes=[("tmp", iv)])
        P.add("act", lambda e: e.activation(out=var, in_=var, func=AF.Sqrt, bias=EPS, scale=1.0),
              reads=[("tmp", iv)], writes=[("tmp", iv)])
        P.add("dve", lambda e: e.reciprocal(out=var, in_=var), reads=[("tmp", iv)], writes=[("tmp", iv)])
        for c in range(CFC):
            Z = self.z[:, c, :]
            P.add("dve", lambda e, Z=Z: e.tensor_tensor(out=Z, in0=Z, in1=mean, op=ALU.subtract),
                  reads=[("z", c), ("tmp", im)], writes=[("z", c)])
            P.add("dve", lambda e, Z=Z: e.tensor_tensor(out=Z, in0=Z, in1=var, op=ALU.mult),
                  reads=[("z", c), ("tmp", iv)], writes=[("z", c)])
            lw, lb = self.pvcol(l, "lnw", c), self.pvcol(l, "lnb", c)
            P.add("act", lambda e, c=c, Z=Z, lw=lw, lb=lb: e.activation(out=self.a_cf[:, c, :], in_=Z, func=AF.Silu, bias=lb, scale=lw),
                  reads=[("z", c), "pv"], writes=[("a_cf", c)])

    def merge_and_out(self, l, t):
        cfg, P = self.cfg, self.P
        DC, T, SCC, CFC, RC = cfg["DC"], cfg["T"], cfg["SCC"], cfg["CFC"], cfg["RC"]
        W = self.W[l]
        KW = DC * 128
        rk = [("hT", d) for d in range(DC)]
        hfn = lambda k: self.hT[:, k, :]
        g0 = [cfg["IN_OFF"][n] // 128 for n in ("gate_sc", "gate_cf", "gate_ret")]
        for d in range(DC):
            s, (o0, o1) = self.load_slab([(W["win"][g0[0] + d], KW), (W["win"][g0[1] + d], KW)])
            bg = [self.lin_fm(s, o0, DC, hfn, rk), self.lin_fm(s, o1, DC, hfn, rk)]
            s2, (p0, p1, p2, p3) = self.load_slab([(W["win"][g0[2] + d], KW), (W["sco"][d], SCC * 128),
                                                   (W["cfo"][d], CFC * 128), (W["reo"][d], RC * 128)])
            bg.append(self.lin_fm(s2, p0, DC, hfn, rk))
            by = [self.lin_fm(s2, p1, SCC, lambda k: self.a_sc[:, k, :], [("a_sc", k) for k in range(SCC)]),
                  self.lin_fm(s2, p2, CFC, lambda k: self.a_cf[:, k, :], [("a_cf", k) for k in range(CFC)]),
                  self.lin_fm(s2, p3, RC, lambda k: self.ogT[:, k, :], [("ogT", k) for k in range(RC)])]
            im = self.tmp()
            M = self.tmpb[:, im, :]
            for i in range(3):
                ti = self.tmp()
                A = self.tmpb[:, ti, :]
                P.add("act", lambda e, A=A, b=bg[i]: e.activation(out=A, in_=self.ps[b], func=AF.Sigmoid),
                      reads=[("ps", bg[i])], writes=[("tmp", ti)])
                if i == 0:
                    P.add("dve", lambda e, A=A, b=by[i], M=M: e.tensor_tensor(out=M, in0=A, in1=self.ps[b], op=ALU.mult),
                          reads=[("tmp", ti), ("ps", by[i])], writes=[("tmp", im)])
                else:
                    P.add("dve", lambda e, A=A, b=by[i]: e.tensor_tensor(out=A, in0=A, in1=self.ps[b], op=ALU.mult),
                          reads=[("tmp", ti), ("ps", by[i])], writes=[("tmp", ti)])
                    if i == 1:
                        P.add("dve", lambda e, A=A, M=M: e.tensor_tensor(out=M, in0=M, in1=A, op=ALU.add),
                              reads=[("tmp", ti), ("tmp", im)], writes=[("tmp", im)])
                    else:
                        P.add("dve", lambda e, A=A, d=d, M=M: e.tensor_tensor(out=self.merged[:, d, :], in0=M, in1=A, op=ALU.add),
                              reads=[("tmp", ti), ("tmp", im)], writes=[("mg", d)])
        for d in range(DC):
            s, (o0,) = self.load_slab([(W["mxo"][d], KW)])
            b = self.lin_fm(s, o0, DC, lambda k: self.merged[:, k, :], [("mg", k) for k in range(DC)])
            xa, xk, xs_ = self.x_stream(t, d)
            P.add("dve", lambda e, xa=xa, b=b: e.tensor_tensor(out=xa, in0=xa, in1=self.ps[b], op=ALU.add),
                  reads=xk + [("ps", b)], writes=xk)
            dst = self.d_xs[d * 128:(d + 1) * 128, t * T:(t + 1) * T]
            self.dma("sp", dst, xa, xk, [("xs", t, d)], ("xsst", xs_))

    def halo_tail(self, l):
        cfg, P = self.cfg, self.P
        DC, T, SCC, CFC = cfg["DC"], cfg["T"], cfg["SCC"], cfg["CFC"]
        W = self.W[l]
        KW = DC * 128
        rk = [("hT", d) for d in range(DC)]
        oc, oh = (cfg["IN_OFF"][n] // 128 for n in ("sc_c", "sc_h"))
        rhs = lambda k: self.hT[:, k, T - 32:T]
        for c in range(SCC):
            s, (o0, o1) = self.load_slab([(W["win"][oc + c], KW), (W["win"][oh + c], KW)])
            bc = self.lin_fm(s, o0, DC, rhs, rk, ncol=32)
            bh = self.lin_fm(s, o1, DC, rhs, rk, ncol=32)
            ti = self.tmp()
            A = self.tmpb[:, ti, 0:32]
            P.add("act", lambda e, bc=bc, A=A: e.copy(out=A, in_=self.ps[bc][:, 0:32]), reads=[("ps", bc)], writes=[("tmp", ti)])
            P.add("dve", lambda e, c=c, A=A, bh=bh: e.tensor_tensor(out=self.hal_sc[:, c, :], in0=A[:, 30:32], in1=self.ps[bh][:, 30:32],
                                                                   op=ALU.mult),
                  reads=[("tmp", ti), ("ps", bh)], writes=[("hal_sc", c)])
        for c in range(CFC):
            bv, ti = self.cf_u(l, c, 32, rhs)
            A = self.tmpb[:, ti, 0:32]
            P.add("dve", lambda e, c=c, A=A, bv=bv: e.tensor_tensor(out=self.hal_cf[:, c, :], in0=A[:, 2:32], in1=self.ps[bv][:, 2:32],
                                                                   op=ALU.mult),
                  reads=[("tmp", ti), ("ps", bv)], writes=[("hal_cf", c)])

    def exo_ap(self, l):
        return self.d_exo[l] if self.fused else self.d_exo

    def exi_ap(self, l):
        if self.fused:
            return self.d_exi[l].rearrange("(s p) w -> s p w", p=128)
        return self.d_exi

    def write_exchange(self, l):
        cfg = self.cfg
        RC, SCC, CFC, H = cfg["RC"], cfg["SCC"], cfg["CFC"], cfg["H"]
        ex = self.exo_ap(l)
        n = RC * 256
        self.dma("sp", ex[:, 0:n], self.d_park, [("park", h) for h in range(H)], [("exo", l, 0)], "exo0")
        self.dma("sp", ex[:, n:n + 2 * SCC], self.hal_sc.rearrange("p c t -> p (c t)"), [("hal_sc", c) for c in range(SCC)],
                 [("exo", l, 1)], "exo1")
        self.dma("sp", ex[:, n + 2 * SCC:n + 2 * SCC + 30 * CFC], self.hal_cf.rearrange("p c t -> p (c t)"),
                 [("hal_cf", c) for c in range(CFC)], [("exo", l, 2)], "exo2")

    def read_exchange(self, l):
        cfg, P = self.cfg, self.P
        RC, SCC, CFC, H = cfg["RC"], cfg["SCC"], cfg["CFC"], cfg["H"]
        ex = self.exi_ap(l)
        n = RC * 256
        deps = [("exo", l, 0), ("exo", l, 1), ("exo", l, 2), ("exg", l)]
        for h in range(H):
            si = self.sh_rr
            self.sh_rr = 1 - si
            Sh = self.Sh[:, si]
            for c in range(2):
                sc = 2 * h + c
                for s in range(4):
                    ti = self.tmp()
                    buf = self.tmpb[:, ti, 0:256]
                    self.dma("sp", buf, ex[s, :, sc * 256:(sc + 1) * 256], deps, [("tmp", ti)], ("exs", ti))
                    cf = self.coef[:, s * H + h:s * H + h + 1]
                    if s == 0:
                        P.add("dve", lambda e, c=c, buf=buf, cf=cf, Sh=Sh: e.tensor_scalar(out=Sh[:, c, :], in0=buf, scalar1=cf,
                                                                                   scalar2=None, op0=ALU.mult),
                              reads=[("tmp", ti), "coef"], writes=[("Sh", si, c)])
                    else:
                        P.add("dve", lambda e, c=c, buf=buf, cf=cf, Sh=Sh: e.scalar_tensor_tensor(out=Sh[:, c, :], in0=buf, scalar=cf,
                                                                                          in1=Sh[:, c, :], op0=ALU.mult, op1=ALU.add),
                              reads=[("tmp", ti), "coef", ("Sh", si, c)], writes=[("Sh", si, c)])
            self.dma("sp", self.d_park[:, 2 * h * 256:(2 * h + 2) * 256], Sh.rearrange("p c t -> p (c t)"),
                     [("Sh", si, 0), ("Sh", si, 1)], [("park", h)], ("Shs", si))
        nh = 2 * SCC + 30 * CFC
        hs = self.hal_sc.rearrange("p c t -> p (c t)")
        hc = self.hal_cf.rearrange("p c t -> p (c t)")
        for s in range(4):
            ti = self.tmp()
            buf = self.tmpb[:, ti, 0:nh]
            self.dma("sp", buf, ex[s, :, n:n + nh], deps, [("tmp", ti)], ("exs", ti))
            sel = self.coef[:, 4 * H + s:4 * H + s + 1]
            for dst, a, b_, kk in ((hs, 0, 2 * SCC, "hal_sc"), (hc, 2 * SCC, nh, "hal_cf")):
                nchunk = SCC if kk == "hal_sc" else CFC
                keys = [(kk, c) for c in range(nchunk)]
                if s == 0:
                    P.add("dve", lambda e, dst=dst, a=a, b_=b_, buf=buf, sel=sel: e.tensor_scalar(out=dst, in0=buf[:, a:b_], scalar1=sel,
                                                                                                 scalar2=None, op0=ALU.mult),
                          reads=[("tmp", ti), "coef"], writes=keys)
                else:
                    P.add("dve", lambda e, dst=dst, a=a, b_=b_, buf=buf, sel=sel: e.scalar_tensor_tensor(out=dst, in0=buf[:, a:b_], scalar=sel,
                                                                                                        in1=dst, op0=ALU.mult, op1=ALU.add),
                          reads=[("tmp", ti), "coef"] + keys, writes=keys)

    def dbg(self, name, ap2d, keys):
        if not self.cfg.get("DBG"):
            return
        n = ap2d.shape[1]
        o = self.dbg_off
        self.dbg_off += n
        self.dbg_map[name] = (o, n)
        self.dma("pool", self.d_dbg[:, o:o + n], ap2d, keys, [("dbg", name)], ("dbg", len(self.dbg_map)))

    def switch(self):
        self.P.add("dve", lambda e: e.memset(self.small[:, 15:16], 0.0), writes=["R1sw"])

    def seg_A(self, l):
        cfg, P = self.cfg, self.P
        NT, DC, H = cfg["NT"], cfg["DC"], cfg["H"]
        W = self.W[l]
        src = self.d_xin if l == 0 else self.d_xs
        for t in range(NT):
            rkeys = [] if l == 0 else [("xs", t, d) for d in range(DC)] + ["xs_all"]
            self.switch()
            self.load_xT(src, t, rkeys)
            self.norm_resident(l, "n1")
            self.ffn(l, W["f1i"], W["f1o"])
            self.store_xT(t)
            self.norm_resident(l, "nm")
            self.switch()
            self.trig_tile(t)
            for h in range(H):
                self.ret_head(l, h, "A", t == 0)
            if t == NT - 1:
                self.halo_tail(l)
        self.write_exchange(l)

    def seg_B(self, l):
        cfg, P = self.cfg, self.P
        NT, DC, H, T = cfg["NT"], cfg["DC"], cfg["H"], cfg["T"]
        W = self.W[l]
        self.read_exchange(l)
        last = (l == cfg["DEPTH"] - 1)
        for t in range(NT):
            self.switch()
            self.norm_stream(l, "nm", t)
            self.trig_tile(t)
            if t == 0:
                self.dbg("hT", self.hT.rearrange("p c t -> p (c t)"), [("hT", d) for d in range(DC)])
                self.dbg("rstd", self.rstd[:, 0, :], [("rstd", 0)])
            self.sc_branch(l, t)
            if t == 0:
                self.dbg("a_sc", self.a_sc.rearrange("p c t -> p (c t)"), [("a_sc", c) for c in range(cfg["SCC"])])
            self.cf_branch(l, t)
            if t == 0:
                self.dbg("a_cf", self.a_cf.rearrange("p c t -> p (c t)"), [("a_cf", c) for c in range(cfg["CFC"])])
            for h in range(H):
                self.ret_head(l, h, "B", t == 0)
            if t == 0:
                self.dbg("ogT", self.ogT.rearrange("p c t -> p (c t)"), [("ogT", c) for c in range(cfg["RC"])])
            self.merge_and_out(l, t)
            if t == 0:
                self.dbg("merged", self.merged.rearrange("p c t -> p (c t)"), [("mg", d) for d in range(DC)])
            self.switch()
            self.load_xT(self.d_xs, t, [("xs", t, d) for d in range(DC)] + ["xs_all"])
            self.norm_resident(l, "n2")
            self.ffn(l, W["f2i"], W["f2o"])
            if last:
                self.rms_stats(lambda d: (self.xT[:, d, :], [("xT", d)]), cfg["D"])
                for d in range(DC):
                    g = self.pvcol(l, "nf", d)
                    s = self.xst_rr
                    self.xst_rr = 1 - s
                    P.add("dve", lambda e, d=d, g=g, s=s: e.scalar_tensor_tensor(out=self.xst[:, s, :], in0=self.xT[:, d, :], scalar=g,
                                                                                in1=self.rstd[:, 0, :], op0=ALU.mult, op1=ALU.mult),
                          reads=[("xT", d), ("rstd", 0), "pv"], writes=[("xst", s)])
                    dst = self.d_out[d * 128:(d + 1) * 128, t * T:(t + 1) * T]
                    self.dma("sp", dst, self.xst[:, s, :], [("xst", s)], [("out", t, d)], ("xsst", s))
            else:
                self.store_xT(t)

    def build(self, cdec):
        cfg = self.cfg
        self.cdec = cdec
        self.alloc()
        DC = cfg["DC"]
        self.declare()
        self.d_park = self.dint("park", [128, cfg["RC"] * 256])
        self.dbg_off = 0
        self.dbg_map = {}
        if cfg.get("DBG"):
            self.d_dbg = self.dout("dbg", [128, cfg["DBG"]])
        self.prologue()
        for kind, l in self.segs:
            if kind == "A":
                self.seg_A(l)
            else:
                self.seg_B(l)
        P = self.P
        outkeys = []
        if self.last:
            outkeys += [("out", t, d) for t in range(cfg["NT"]) for d in range(DC)]
        else:
            outkeys += [("xs", t, d) for t in range(cfg["NT"]) for d in range(DC)]
        if any(k == "A" for k, _ in self.segs):
            l = [ll for k, ll in self.segs if k == "A"][-1]
            outkeys += [("exo", l, i) for i in range(3)]
        outkeys += [("dbg", nm) for nm in self.dbg_map]
        P.add("sp", lambda e: None, reads=outkeys)
        names = P.finalize()
        nc = self.nc
        sems = {n: nc.alloc_semaphore("s%d" % i) for i, n in enumerate(names)}
        self.n_sems = len(names)
        with nc.Block() as block:
            @block.sync
            def _(e):
                P.emit_engine("sp", e, sems)

            @block.scalar
            def _(e):
                P.emit_engine("act", e, sems)

            @block.vector
            def _(e):
                P.emit_engine("dve", e, sems)

            @block.gpsimd
            def _(e):
                P.emit_engine("pool", e, sems)

            @block.tensor
            def _(e):
                P.emit_engine("pe", e, sems)
        return nc


def tile_w(W, cw):
    K, M = W.shape
    KT = K // 128
    return np.ascontiguousarray(W.reshape(KT, 128, M // cw, cw).transpose(2, 1, 0, 3)).reshape(M // cw, 128, KT * cw)


def prep_layer(cfg, inp, l):
    H = cfg["H"]
    o = {}
    o["f1i_%d" % l] = tile_w(inp["ffn1_in"][l], 128)
    o["f1o_%d" % l] = tile_w(inp["ffn1_out"][l], 128)
    o["f2i_%d" % l] = tile_w(inp["ffn2_in"][l], 128)
    o["f2o_%d" % l] = tile_w(inp["ffn2_out"][l], 128)
    win = inp["w_in"][l]
    o["win_%d" % l] = tile_w(win, 128)
    ov, og = cfg["IN_OFF"]["v"], cfg["IN_OFF"]["g"]
    o["wvg_%d" % l] = np.concatenate([tile_w(win[:, ov:ov + cfg["RW"]], 256), tile_w(win[:, og:og + cfg["RW"]], 256)], axis=0)
    o["sco_%d" % l] = tile_w(inp["w_sc_out"][l], 128)
    o["cfo_%d" % l] = tile_w(inp["w_cf_out"][l], 128)
    o["reo_%d" % l] = tile_w(inp["w_ret_out"][l], 128)
    o["mxo_%d" % l] = tile_w(inp["w_mix_out"][l], 128)
    return o


def prep_small(cfg, inp):
    DC, SCC, CFC, L = cfg["DC"], cfg["SCC"], cfg["CFC"], cfg["DEPTH"]
    NPV = 3 * DC + 3 * SCC + 34 * CFC
    pv = np.zeros((128, L * NPV + DC), np.float32)

    def col(v):
        return v.reshape(-1, 128).T
    for l in range(L):
        b = l * NPV
        pv[:, b:b + DC] = col(inp["norm_ffn1"][l]); b += DC
        pv[:, b:b + DC] = col(inp["norm_mix"][l]); b += DC
        pv[:, b:b + DC] = col(inp["norm_ffn2"][l]); b += DC
        w = inp["sc_conv_w"][l]
        pv[:, b:b + 3 * SCC] = w.reshape(3, SCC, 128).transpose(2, 1, 0).reshape(128, 3 * SCC); b += 3 * SCC
        w = inp["cf_dw_w"][l]
        pv[:, b:b + 31 * CFC] = w.reshape(31, CFC, 128).transpose(2, 1, 0).reshape(128, 31 * CFC); b += 31 * CFC
        pv[:, b:b + CFC] = col(inp["cf_dw_b"][l]); b += CFC
        pv[:, b:b + CFC] = col(inp["cf_ln_w"][l]); b += CFC
        pv[:, b:b + CFC] = col(inp["cf_ln_b"][l]); b += CFC
    pv[:, L * NPV:] = col(inp["norm_final"])
    return pv


def coef_table(cfg, core, lg, fused):
    H, CPB, TOK = cfg["H"], cfg["CPB"], cfg["TOK"]
    r = core % CPB
    co = np.zeros((128, 8 * H + 8), np.float32)
    for s in range(4):
        if s < r:
            co[:, s * H:(s + 1) * H] = np.exp(np.float32(TOK * (r - s - 1)) * lg.astype(np.float64)).astype(np.float32)[None, :]
        if s == r - 1:
            co[:, 4 * H + s] = 1.0
    return co


_PROG_CACHE = {}


def get_prog(cfg, segs, fused, cdec):
    key = (tuple(sorted((k, str(v)) for k, v in cfg.items())), tuple(segs), fused)
    if key not in _PROG_CACHE:
        b = Builder(cfg, segs, fused)
        _PROG_CACHE[key] = b.build(cdec)
    return _PROG_CACHE[key]


def run_unfused(cfg, inp):
    NC, CPB, TOK, D, L = cfg["NCORES"], cfg["CPB"], cfg["TOK"], cfg["D"], cfg["DEPTH"]
    tabs, cdec, lg = const_tables(cfg)
    x = inp["x"]
    pos = inp["positions"]
    pv = prep_small(cfg, inp)
    gnw = np.ascontiguousarray(inp["ret_gn_w"])
    ident = np.eye(128, dtype=np.float32)
    xin = [np.ascontiguousarray(x[c // CPB, (c % CPB) * TOK:(c % CPB + 1) * TOK, :].T) for c in range(NC)]
    posc = [np.ascontiguousarray(pos[c // CPB, (c % CPB) * TOK:(c % CPB + 1) * TOK].reshape(1, TOK)).astype(np.int32)
            for c in range(NC)]
    common = dict(pv=pv, gnw=gnw, tabs=tabs, ident=ident)
    coefs = [coef_table(cfg, c, lg, False) for c in range(NC)]
    seglists = [[("A", 0)]]
    for l in range(L):
        if l + 1 < L:
            seglists.append([("B", l), ("A", l + 1)])
        else:
            seglists.append([("B", l)])
    xs = None
    exo = None
    out = None
    wcache = {}
    for segs in seglists:
        nc = get_prog(cfg, segs, False, cdec)
        layers = sorted({l for _, l in segs})
        wl = {}
        for l in layers:
            if l not in wcache:
                wcache[l] = prep_layer(cfg, inp, l)
            kinds = {k for k, ll in segs if ll == l}
            for nm, v in wcache[l].items():
                base = nm.rsplit("_", 1)[0]
                if base in ("f1i", "f1o") and "A" not in kinds:
                    continue
                if base in ("sco", "cfo", "reo", "mxo", "f2i", "f2o") and "B" not in kinds:
                    continue
                wl[nm] = v
        in_maps = []
        for c in range(NC):
            m = dict(common)
            m.update(wl)
            m["coef"] = coefs[c]
            m["pos"] = posc[c]
            if segs[0] == ("A", 0):
                m["xin"] = xin[c]
            else:
                m["xsin"] = xs[c]
                b0 = (c // CPB) * CPB
                m["exi"] = np.stack([exo[b0 + s] for s in range(4)], axis=0)
            in_maps.append(m)
        res = run_bass_kernel_spmd(nc, in_maps, core_ids=list(range(NC)))
        r = res.results
        if segs[-1] == ("B", L - 1):
            out = [r[c]["out"] for c in range(NC)]
        else:
            xs = [r[c]["xs"] for c in range(NC)]
            exo = [r[c]["exo"] for c in range(NC)]
    y = np.empty((cfg["BATCH"], cfg["SEQ"], D), np.float32)
    for c in range(NC):
        y[c // CPB, (c % CPB) * TOK:(c % CPB + 1) * TOK, :] = out[c].T
    return y


def kernel(**inputs):
    cfg = make_cfg()
    inp = {k: np.asarray(v) for k, v in inputs.items()}
    return run_unfused(cfg, inp)
```
